# Optimizing a Trainium2 kernel written in Bass

```python
import math
import jax, jax.numpy as jnp
from jax import lax
import numpy as np

D_MODEL = 4096
BATCH = 4
SEQ = 2048
DEPTH = 2
DEC_BATCH = 32
DEC_SEQ = 1
PAST_LEN = 16384
PAGE_SIZE = 128

N_META = 16
N_MIXERS = 2
N_LAYERS_A = (DEPTH + 1) // 2
N_LAYERS_B = DEPTH // 2
HG_EXPAND = 128
HG_HEADS = D_MODEL // HG_EXPAND
HG_DK = HG_EXPAND
HG_DV = D_MODEL // HG_HEADS
HG_F = HG_HEADS * HG_DK
HG_CHUNK = 16
SW_HEADS = 64
SW_KV = 8
SW_HD = D_MODEL // SW_HEADS
SW_GROUP = SW_HEADS // SW_KV
WINDOW = 128
SW_BLOCK = 128
REL_BUCKETS = 32
REL_MAX_DIST = 128
EPS = 1e-6
NEG = -1e30

kernel_name = "hgrn2_swa_sink_hybrid_step"


def rms_norm(x, g):
    xf = x.astype(jnp.float32)
    y = xf * lax.rsqrt(jnp.mean(xf * xf, axis=-1, keepdims=True) + EPS)
    return (y * g.astype(jnp.float32)).astype(x.dtype)


def t5_bucket(dist):
    max_exact = REL_BUCKETS // 2
    d = jnp.maximum(dist, 1).astype(jnp.float32)
    large = max_exact + (jnp.log(d / max_exact) / math.log(REL_MAX_DIST / max_exact)
                         * (REL_BUCKETS - max_exact)).astype(jnp.int32)
    large = jnp.minimum(large, REL_BUCKETS - 1)
    return jnp.where(dist < max_exact, dist, large)


def rel_bias_heads(dist, table):
    b = table.astype(jnp.float32)[t5_bucket(jnp.maximum(dist, 0))]
    b = jnp.transpose(b, (2, 0, 1))
    return b.reshape(SW_KV, SW_GROUP, dist.shape[0], dist.shape[1])


def hgrn_project(h, w_in, lb):
    B, T, _ = h.shape
    proj = h @ w_in
    q = proj[..., :HG_F]
    fp = proj[..., HG_F:2 * HG_F].astype(jnp.float32)
    i = proj[..., 2 * HG_F:3 * HG_F]
    g = proj[..., 3 * HG_F:]
    q = jax.nn.silu(q.astype(jnp.float32)).reshape(B, T, HG_HEADS, HG_DK)
    f = lb + (1.0 - lb) * jax.nn.sigmoid(fp)
    logf = jnp.log(f).reshape(B, T, HG_HEADS, HG_DK)
    k = ((1.0 - lb) * jax.nn.sigmoid(-fp)).reshape(B, T, HG_HEADS, HG_DK)
    v = i.astype(jnp.float32).reshape(B, T, HG_HEADS, HG_DV)
    return q, k, v, logf, g


def hgrn_chunked(q, k, v, logf, S0):
    B, T, H, _ = q.shape
    N = T // HG_CHUNK
    def r(a):
        return a.reshape(B, N, HG_CHUNK, H, a.shape[-1]).transpose(1, 0, 3, 2, 4)
    causal = jnp.tril(jnp.ones((HG_CHUNK, HG_CHUNK), dtype=bool))
    def step(S, xs):
        qc, kc, vc, lc = xs
        b = jnp.cumsum(lc, axis=2)
        qt = qc * jnp.exp(b)
        kt = kc * jnp.exp(-b)
        A = jnp.where(causal, jnp.einsum('bhtk,bhsk->bhts', qt, kt), 0.0)
        o = jnp.einsum('bhtk,bhkv->bhtv', qt, S) + jnp.einsum('bhts,bhsv->bhtv', A, vc)
        bl = b[:, :, -1:, :]
        S = jnp.exp(bl[:, :, 0, :])[..., None] * S + jnp.einsum('bhsk,bhsv->bhkv', kc * jnp.exp(bl - b), vc)
        return S, o
    S, o = lax.scan(step, S0, (r(q), r(k), r(v), r(logf)))
    o = o.transpose(1, 0, 3, 2, 4).reshape(B, T, H, HG_DV)
    return o, S


def hgrn_recurrent(q, k, v, logf, S0):
    def step(S, xs):
        qt, kt, vt, lt = xs
        S = jnp.exp(lt)[..., None] * S + kt[..., None] * vt[..., None, :]
        return S, jnp.einsum('bhk,bhkv->bhv', qt, S)
    tr = lambda a: a.transpose(1, 0, 2, 3)
    S, o = lax.scan(step, S0, (tr(q), tr(k), tr(v), tr(logf)))
    return tr(o), S


def hgrn_out(o, g, onorm, w_out):
    B, T = o.shape[:2]
    o = o * lax.rsqrt(jnp.mean(o * o, axis=-1, keepdims=True) + EPS)
    o = o.reshape(B, T, D_MODEL) * onorm.astype(jnp.float32)
    return (o.astype(g.dtype) * jax.nn.silu(g)) @ w_out


def swa_project(h, w_in):
    B, T, _ = h.shape
    proj = h @ w_in
    nq, nk = SW_HEADS * SW_HD, SW_KV * SW_HD
    q = proj[..., :nq].reshape(B, T, SW_HEADS, SW_HD)
    k = proj[..., nq:nq + nk].reshape(B, T, SW_KV, SW_HD)
    v = proj[..., nq + nk:nq + 2 * nk].reshape(B, T, SW_KV, SW_HD)
    g = proj[..., nq + 2 * nk:]
    return q, k, v, g


def sink_softmax_av(s, sinks, v):
    sk = sinks.astype(jnp.float32).reshape(SW_KV, SW_GROUP, 1, 1)
    m = jnp.maximum(jnp.max(s, axis=-1, keepdims=True), sk)
    p = jnp.exp(s - m)
    den = jnp.sum(p, axis=-1, keepdims=True) + jnp.exp(sk - m)
    return jnp.einsum('bngqk,bnkd->bngqd', p / den, v.astype(jnp.float32))


def swa_prompt(q, k, v, sinks, table):
    B, T = q.shape[:2]
    pad = (-T) % SW_BLOCK
    Tp = T + pad
    nb = Tp // SW_BLOCK
    scale = SW_HD ** -0.5
    padf = lambda a: jnp.pad(a, ((0, 0), (pad, 0), (0, 0), (0, 0)))
    qb = padf(q).reshape(B, nb, SW_BLOCK, SW_KV, SW_GROUP, SW_HD).transpose(1, 0, 3, 4, 2, 5)
    def band(a):
        ab = padf(a).reshape(B, nb, SW_BLOCK, SW_KV, SW_HD)
        prev = jnp.concatenate([jnp.zeros_like(ab[:, :1]), ab[:, :-1]], axis=1)
        return jnp.concatenate([prev, ab], axis=2).transpose(1, 0, 3, 2, 4)
    kband, vband = band(k), band(v)
    key_pos = (jnp.arange(nb)[:, None] - 1) * SW_BLOCK + jnp.arange(2 * SW_BLOCK)[None, :]
    key_valid = key_pos >= pad
    dist = jnp.arange(SW_BLOCK)[:, None] + SW_BLOCK - jnp.arange(2 * SW_BLOCK)[None, :]
    in_win = (dist >= 0) & (dist <= WINDOW)
    bias = rel_bias_heads(dist, table)
    def block(xs):
        qk, kk, vk, kvv = xs
        s = jnp.einsum('bngqd,bnkd->bngqk', qk, kk).astype(jnp.float32) * scale + bias
        s = jnp.where(in_win & kvv[None, :], s, NEG)
        return sink_softmax_av(s, sinks, vk)
    o = lax.map(block, (qb, kband, vband, key_valid))
    o = o.transpose(1, 0, 4, 2, 3, 5).reshape(B, Tp, SW_HEADS, SW_HD)
    return o[:, pad:]


def swa_sample(q, k, v, ck, cv, sinks, table):
    B, T = q.shape[:2]
    R = ck.shape[1]
    kall = jnp.concatenate([ck.astype(k.dtype), k], axis=1)
    vall = jnp.concatenate([cv.astype(v.dtype), v], axis=1)
    qpos = PAST_LEN + jnp.arange(T)
    kpos = jnp.concatenate([PAST_LEN - R + jnp.arange(R), PAST_LEN + jnp.arange(T)])
    dist = qpos[:, None] - kpos[None, :]
    in_win = (dist >= 0) & (dist <= WINDOW)
    bias = rel_bias_heads(dist, table)
    qg = q.reshape(B, T, SW_KV, SW_GROUP, SW_HD)
    s = jnp.einsum('bqngd,bknd->bngqk', qg, kall).astype(jnp.float32) * (SW_HD ** -0.5) + bias
    s = jnp.where(in_win, s, NEG)
    o = sink_softmax_av(s, sinks, vall.transpose(0, 2, 1, 3))
    o = o.transpose(0, 3, 1, 2, 4).reshape(B, T, SW_HEADS, SW_HD)
    return o, kall[:, -R:], vall[:, -R:]


def swa_out(o, g, w_out):
    B, T = o.shape[:2]
    return (o.reshape(B, T, D_MODEL).astype(g.dtype) * jax.nn.silu(g)) @ w_out


def setup_inputs(seed: int = 0) -> dict:
    key = jax.random.key(seed)
    ks = jax.random.split(key, 18)
    nrm = lambda k, shp, s=1.0: jax.random.normal(k, shp, jnp.float32) * s
    win_rows = min(WINDOW, PAST_LEN)
    sw_in_cols = SW_HEADS * SW_HD + 2 * SW_KV * SW_HD + D_MODEL
    hg_in_cols = 3 * HG_F + D_MODEL
    return {
        "x_prompt": nrm(ks[0], (BATCH, SEQ, D_MODEL)),
        "x_sample": nrm(ks[1], (DEC_BATCH, DEC_SEQ, D_MODEL)),
        "state_hgrn": nrm(ks[2], (N_LAYERS_A, DEC_BATCH, HG_HEADS, HG_DK, HG_DV), 0.5),
        "cache_k_win": nrm(ks[3], (N_LAYERS_B, DEC_BATCH, win_rows, SW_KV, SW_HD)),
        "cache_v_win": nrm(ks[4], (N_LAYERS_B, DEC_BATCH, win_rows, SW_KV, SW_HD)),
        "meta_tokens": nrm(ks[5], (N_META, D_MODEL)),
        "rel_bias": nrm(ks[6], (REL_BUCKETS, SW_HEADS), 0.5),
        "hg_lower_bounds": nrm(ks[7], (DEPTH + 1, HG_F), 0.5),
        "hg_norm": 1.0 + nrm(ks[8], (N_LAYERS_A, D_MODEL), 0.02),
        "hg_w_in": nrm(ks[9], (N_LAYERS_A, D_MODEL, hg_in_cols), D_MODEL ** -0.5),
        "hg_onorm": 1.0 + nrm(ks[10], (N_LAYERS_A, D_MODEL), 0.02),
        "hg_w_out": nrm(ks[11], (N_LAYERS_A, D_MODEL, D_MODEL), D_MODEL ** -0.5),
        "sw_norm": 1.0 + nrm(ks[12], (N_LAYERS_B, D_MODEL), 0.02),
        "sw_w_in": nrm(ks[13], (N_LAYERS_B, D_MODEL, sw_in_cols), D_MODEL ** -0.5),
        "sw_sinks": nrm(ks[14], (N_LAYERS_B, SW_HEADS), 0.5),
        "sw_w_out": nrm(ks[15], (N_LAYERS_B, D_MODEL, D_MODEL), D_MODEL ** -0.5),
        "final_norm": 1.0 + nrm(ks[16], (D_MODEL,), 0.02),
    }


def reference(x_prompt, x_sample, state_hgrn, cache_k_win, cache_v_win, meta_tokens, rel_bias,
              hg_lower_bounds, hg_norm, hg_w_in, hg_onorm, hg_w_out,
              sw_norm, sw_w_in, sw_sinks, sw_w_out, final_norm):
    lb_all = jnp.cumsum(jax.nn.softmax(hg_lower_bounds.astype(jnp.float32), axis=0), axis=0)
    meta = jnp.broadcast_to(meta_tokens.astype(x_prompt.dtype)[None], (x_prompt.shape[0], N_META, D_MODEL))
    xp = jnp.concatenate([meta, x_prompt], axis=1)
    xs = x_sample
    hg_sp, hg_ss, kp_l, vp_l, ks_l, vs_l = [], [], [], [], [], []
    for i in range(DEPTH):
        if i % N_MIXERS == 0:
            a = i // N_MIXERS
            lb = lb_all[i]
            hp = rms_norm(xp, hg_norm[a])
            q, k, v, lf, g = hgrn_project(hp, hg_w_in[a], lb)
            S0 = jnp.zeros((xp.shape[0], HG_HEADS, HG_DK, HG_DV), jnp.float32)
            o, Sp = hgrn_chunked(q, k, v, lf, S0)
            xp = xp + hgrn_out(o, g, hg_onorm[a], hg_w_out[a])
            hs = rms_norm(xs, hg_norm[a])
            q, k, v, lf, g = hgrn_project(hs, hg_w_in[a], lb)
            o, Ss = hgrn_recurrent(q, k, v, lf, state_hgrn[a].astype(jnp.float32))
            xs = xs + hgrn_out(o, g, hg_onorm[a], hg_w_out[a])
            hg_sp.append(Sp.astype(state_hgrn.dtype))
            hg_ss.append(Ss.astype(state_hgrn.dtype))
        else:
            b = i // N_MIXERS
            hp = rms_norm(xp, sw_norm[b])
            q, k, v, g = swa_project(hp, sw_w_in[b])
            o = swa_prompt(q, k, v, sw_sinks[b], rel_bias)
            xp = xp + swa_out(o, g, sw_w_out[b])
            kp_l.append(k[:, -WINDOW:].astype(cache_k_win.dtype))
            vp_l.append(v[:, -WINDOW:].astype(cache_v_win.dtype))
            hs = rms_norm(xs, sw_norm[b])
            q, k, v, g = swa_project(hs, sw_w_in[b])
            o, nk, nv = swa_sample(q, k, v, cache_k_win[b], cache_v_win[b], sw_sinks[b], rel_bias)
            xs = xs + swa_out(o, g, sw_w_out[b])
            ks_l.append(nk.astype(cache_k_win.dtype))
            vs_l.append(nv.astype(cache_v_win.dtype))
    y_prompt = rms_norm(xp, final_norm)[:, N_META:]
    y_sample = rms_norm(xs, final_norm)
    return (y_prompt, y_sample, jnp.stack(hg_sp), jnp.stack(kp_l), jnp.stack(vp_l),
            jnp.stack(hg_ss), jnp.stack(ks_l), jnp.stack(vs_l))
```

```python
import numpy as np
from contextlib import ExitStack
import concourse.bass as bass
import concourse.mybir as mybir
from concourse.bass_utils import run_bass_kernel_spmd

F32 = mybir.dt.float32
BF16 = mybir.dt.bfloat16
U8 = mybir.dt.uint8
ALU = mybir.AluOpType
AF = mybir.ActivationFunctionType
AX = mybir.AxisListType

D = 4096
NT = 1152
NS = 4
NTOK = NT + NS
NPRE = 1024
KC = 32
EPS = 1e-6
SCALE = 0.125
THIRDS = [(0, 384, 384), (384, 384, 384), (768, 388, 384)]
PTHIRDS = [(0, 384, 384), (384, 384, 384), (768, 256, 256)]


class Buf:
    __slots__ = ("name", "w", "r", "excl")

    def __init__(self, name="", excl=False):
        self.name = name
        self.w = None
        self.r = {}
        self.excl = excl


class _Op:
    __slots__ = ("eng", "fn", "deps", "dma", "need", "sem", "val")

    def __init__(self, eng, fn, dma):
        self.eng = eng
        self.fn = fn
        self.dma = dma
        self.deps = []
        self.need = dma
        self.sem = None
        self.val = 0


class Prog:
    ENGS = ("pe", "act", "dve", "pool", "sp")
    SEM_LIMIT = 30000

    def __init__(self, nc, n_dma_sems=48):
        self.nc = nc
        self.ops = []
        self.n_dma_sems = n_dma_sems
        self.out_dma = []
        self.last_eng = {}
        self.dma_since = []

    def op(self, eng, fn, reads=(), writes=(), dma=False, out=False, extra=()):
        o = _Op(eng, fn, dma)
        idx = len(self.ops)
        deps = set(extra)
        writes = list(writes) + [b for b in reads if b.excl]
        reads = [b for b in reads if not b.excl]
        for b in reads:
            if b.w is not None:
                deps.add(b.w)
        for b in writes:
            if b.w is not None:
                deps.add(b.w)
            for v in b.r.values():
                deps.add(v)
        for b in reads:
            b.r[("d", idx) if dma else eng] = idx
        for b in writes:
            b.w = idx
            b.r = {}
        for d in deps:
            p = self.ops[d]
            if (not dma) and (not p.dma) and p.eng == eng and eng == "pe":
                continue
            o.deps.append(d)
        self.ops.append(o)
        if dma:
            self.dma_since.append(idx)
        else:
            self.last_eng[eng] = idx
        if out:
            self.out_dma.append(idx)
        return idx

    def pe(self, fn, reads=(), writes=()):
        return self.op("pe", fn, reads, writes)

    def act(self, fn, reads=(), writes=()):
        return self.op("act", fn, reads, writes)

    def dve(self, fn, reads=(), writes=()):
        return self.op("dve", fn, reads, writes)

    def pool(self, fn, reads=(), writes=()):
        return self.op("pool", fn, reads, writes)

    def dma(self, q, out_ap, in_ap, reads=(), writes=(), out=False, **kw):
        return self.op(q, lambda e: e.dma_start(out=out_ap, in_=in_ap, **kw), reads, writes,
                       dma=True, out=out)

    def barrier(self):
        ex = list(self.last_eng.values()) + list(self.dma_since)
        x = self.op("sp", lambda e: e.nop(), extra=ex)
        self.dma_since = []
        for en in ("pe", "act", "dve", "pool"):
            self.op(en, lambda e: e.nop(), extra=[x])

    def build(self):
        nc = self.nc
        ops = self.ops
        last_on = [None] * self.n_dma_sems
        k = 0
        dma_slot = {}
        for i, o in enumerate(ops):
            if o.dma:
                s = k % self.n_dma_sems
                k += 1
                if last_on[s] is not None:
                    o.deps.append(last_on[s])
                last_on[s] = i
                dma_slot[i] = s
        for o in ops:
            for d in o.deps:
                ops[d].need = True
        for d in self.out_dma:
            ops[d].need = True
        es = ExitStack()
        cnt = {e: 0 for e in self.ENGS}
        for o in ops:
            if o.need and not o.dma:
                cnt[o.eng] += 1
        eng_sems = {}
        for e in self.ENGS:
            n = max(1, -(-cnt[e] // self.SEM_LIMIT))
            eng_sems[e] = [es.enter_context(nc.semaphore(f"s_{e}{j}")) for j in range(n)]
        dma_sems = [es.enter_context(nc.semaphore(f"s_d{j}")) for j in range(self.n_dma_sems)]
        dcum = [0] * self.n_dma_sems
        ecnt = {e: 0 for e in self.ENGS}
        for i, o in enumerate(ops):
            if o.dma:
                s = dma_slot[i]
                dcum[s] += 16
                o.sem = dma_sems[s]
                o.val = dcum[s]
            elif o.need:
                c = ecnt[o.eng]
                ecnt[o.eng] += 1
                o.sem = eng_sems[o.eng][c // self.SEM_LIMIT]
                o.val = c % self.SEM_LIMIT + 1
        per = {e: [] for e in self.ENGS}
        for i, o in enumerate(ops):
            per[o.eng].append(i)
        final = [(ops[d].sem, ops[d].val) for d in self.out_dma]

        def emit(eng_name, eng):
            known = {}
            for i in per[eng_name]:
                o = ops[i]
                for d in o.deps:
                    p = ops[d]
                    key = id(p.sem)
                    if known.get(key, 0) < p.val:
                        eng.wait_ge(p.sem, p.val)
                        known[key] = p.val
                ins = o.fn(eng)
                if o.need:
                    ins.then_inc(o.sem, 16 if o.dma else 1)
            if eng_name == "sp":
                for (s, v) in final:
                    if known.get(id(s), 0) < v:
                        eng.wait_ge(s, v)
                        known[id(s)] = v

        block = es.enter_context(nc.Block())

        @block.tensor
        def _(e):
            emit("pe", e)

        @block.scalar
        def _(e):
            emit("act", e)

        @block.vector
        def _(e):
            emit("dve", e)

        @block.gpsimd
        def _(e):
            emit("pool", e)

        @block.sync
        def _(e):
            emit("sp", e)

        es.close()
        return {e: len(per[e]) for e in self.ENGS}


class Arena:
    def __init__(self, ar, size):
        self.ar = ar
        self.size = size
        self.off = 0

    def alloc(self, free, dt, parts=128):
        esz = 4 if dt == F32 else 2
        n = esz
        for f in free:
            n *= f
        off = (self.off + 63) // 64 * 64
        assert off + n <= self.size, ("arena overflow", off, n, self.size)
        self.off = off + n
        ap = self.ar[0:parts, off:off + n].bitcast(dt)
        if len(free) == 2:
            ap = ap.rearrange("p (a b) -> p a b", b=free[1])
        elif len(free) == 3:
            ap = ap.rearrange("p (a b c) -> p a b c", b=free[1], c=free[2])
        return ap

    def mark(self):
        return self.off

    def release(self, m):
        self.off = m


DBG = {"heads": 32, "cut": 9}


def build_program(stop=None):
    nc = bass.Bass("TRN2", target_bir_lowering=False)
    P = Prog(nc)
    es = ExitStack()

    def din(name, shape, dt=F32):
        return nc.dram_tensor(name, shape, dt, kind="ExternalInput").ap()

    def dout(name, shape, dt=F32):
        return nc.dram_tensor(name, shape, dt, kind="ExternalOutput").ap()

    def dscr(name, shape, dt=F32):
        return nc.dram_tensor(name, shape, dt).ap()

    xm = din("xm", [NT, D]); xpre = din("xpre", [NPRE, D]); xs = din("xs", [NS, D])
    st_in = din("st_in", [NS, 32, 128, 128]); ck = din("ck", [NS, 128, 512]); cv = din("cv", [NS, 128, 512])
    w0in = din("w0in", [128, 128, D]); w0out = din("w0out", [8, 128, 16384])
    w1in = din("w1in", [76, 128, D]); w1out = din("w1out", [8, 128, 16384])
    vecs_d = din("vecs", [128, 192]); fnorm_d = din("fnorm", [1, D]); relb_d = din("relb", [32, 64])
    sinkc_d = din("sinkc", [64, 1]); sinkr_d = din("sinkr", [1, 64])
    c_identf = din("c_identf", [128, 128]); c_anti = din("c_anti", [128, 128]); c_cmask = din("c_cmask", [128, 128])
    c_rmask = din("c_rmask", [128, 388]); c_onehot = din("c_onehot", [32, 384]); c_win = din("c_win", [64, 384])
    c_fmask = din("c_fmask", [128, 256]); c_hsel = din("c_hsel", [64, 8])
    y_o = dout("y", [1024, D]); ys_o = dout("ys", [NS, D]); stp_o = dout("stp", [32, 128, 128])
    kp_o = dout("kp", [128, 512]); vp_o = dout("vp", [128, 512]); sts_o = dout("sts", [NS, 32, 128, 128])
    ks_o = dout("ks", [NS, 128, 512]); vs_o = dout("vs", [NS, 128, 512])
    x1 = dscr("x1", [NTOK, D]); x2 = dscr("x2", [NTOK, D]); ogT = dscr("ogT", [32, 128, NTOK], BF16)
    spre = dscr("spre", [32, 128, 128]); ub = dscr("ub", [64, 384])
    qs_s = dscr("qs_s", [NS, D]); kn_s = dscr("kn_s", [NS, 512]); vn_s = dscr("vn_s", [NS, 512]); os_s = dscr("os_s", [NS, D])
    b_x1 = Buf(); b_x2 = Buf(); b_ogT = Buf(); b_spre = [Buf() for _ in range(32)]; b_ub = Buf()
    b_qs = Buf(); b_kn = Buf(); b_vn = Buf(); b_os = Buf()

    ARENA = 207 * 1024
    ar_t = es.enter_context(nc.sbuf_tensor("arena", [128, ARENA], U8))
    A = Arena(ar_t, ARENA)
    pb = [es.enter_context(nc.psum_tensor(f"pb{i}", [128, 512], F32)) for i in range(8)]
    bpb = [Buf(f"pb{i}", excl=True) for i in range(8)]
    pbb = [pb[i][:].bitcast(BF16) for i in range(8)]

    identf = A.alloc([128], F32); identb = A.alloc([128], BF16); onesm = A.alloc([128], F32); ones1 = A.alloc([128], F32)
    anti = A.alloc([128], F32); cmask = A.alloc([128], F32); rmask = A.alloc([388], F32)
    vecs = A.alloc([192], F32); lb = A.alloc([32], F32); oml = A.alloc([32], F32)
    fmask = A.alloc([256], F32); hsel = A.alloc([8], F32, parts=64)
    sinkb = A.alloc([64], F32); sinkc = A.alloc([1], F32, parts=64)
    bC = Buf("consts")
    for dst, src in ((identf, c_identf), (anti, c_anti), (cmask, c_cmask), (rmask, c_rmask), (vecs, vecs_d),
                     (fmask, c_fmask), (hsel, c_hsel), (sinkc, sinkc_d)):
        P.dma("sp", dst, src, writes=[Buf()])
    P.dma("sp", sinkb, sinkr_d.partition_broadcast(128), writes=[Buf()])
    P.pool(lambda e: e.memset(onesm, 1.0 / 128.0), writes=[bC])
    P.pool(lambda e: e.memset(ones1, 1.0), writes=[bC])
    P.barrier()
    P.dve(lambda e: e.tensor_copy(out=identb, in_=identf), writes=[bC])
    tmpc = A.alloc([3, 32], F32)
    lbr = vecs[:, 0:96].rearrange("p (l k) -> p l k", k=32)
    tm = A.alloc([32], F32)
    P.dve(lambda e: e.tensor_tensor(out=tm, in0=lbr[:, 0, :], in1=lbr[:, 1, :], op=ALU.max), writes=[bC])
    P.dve(lambda e: e.tensor_tensor(out=tm, in0=tm, in1=lbr[:, 2, :], op=ALU.max), reads=[bC], writes=[bC])
    P.dve(lambda e: e.tensor_tensor(out=tmpc, in0=lbr, in1=tm.unsqueeze(1).to_broadcast([128, 3, 32]), op=ALU.subtract),
          reads=[bC], writes=[bC])
    P.act(lambda e: e.activation(out=tmpc, in_=tmpc, func=AF.Exp), reads=[bC], writes=[bC])
    P.dve(lambda e: e.tensor_tensor(out=tm, in0=tmpc[:, 0, :], in1=tmpc[:, 1, :], op=ALU.add), reads=[bC], writes=[bC])
    P.dve(lambda e: e.tensor_tensor(out=tm, in0=tm, in1=tmpc[:, 2, :], op=ALU.add), reads=[bC], writes=[bC])
    P.dve(lambda e: e.reciprocal(out=tm, in_=tm), reads=[bC], writes=[bC])
    P.dve(lambda e: e.tensor_tensor(out=lb, in0=tmpc[:, 0, :], in1=tm, op=ALU.mult), reads=[bC], writes=[bC])
    P.dve(lambda e: e.tensor_scalar(out=oml, in0=lb, scalar1=-1.0, scalar2=1.0, op0=ALU.mult, op1=ALU.add),
          reads=[bC], writes=[bC])
    P.barrier()
    g_hg = vecs[:, 96:128]; g_on = vecs[:, 128:160]; g_sw = vecs[:, 160:192]

    bigA = A.alloc([KC, NTOK], BF16)
    bA = Buf("bigA")
    base_mark = A.mark()

    def phase_norm(tiles, gvec):
        m = A.mark()
        XT = [A.alloc([D], F32) for _ in range(2)]; XN = [A.alloc([D], BF16) for _ in range(2)]
        ss = [A.alloc([1], F32) for _ in range(2)]; rs = [A.alloc([1], F32) for _ in range(2)]
        bXT = [Buf() for _ in range(2)]; bXN = [Buf() for _ in range(2)]; bs = [Buf() for _ in range(2)]
        trk = 0
        for i, (src, r, col0, rb) in enumerate(tiles):
            s = i % 2
            P.dma("sp", XT[s][0:r], src, reads=rb, writes=[bXT[s]])
            P.act(lambda e, s=s, r=r: e.activation(out=XN[s][0:r], in_=XT[s][0:r], func=AF.Square, accum_out=ss[s][0:r]),
                  reads=[bXT[s]], writes=[bXN[s], bs[s]])
            P.dve(lambda e, s=s, r=r: e.tensor_scalar(out=rs[s][0:r], in0=ss[s][0:r], scalar1=1.0 / D, scalar2=EPS,
                                                    op0=ALU.mult, op1=ALU.add), reads=[bs[s]], writes=[bs[s]])
            P.act(lambda e, s=s, r=r: e.activation(out=rs[s][0:r], in_=rs[s][0:r], func=AF.Sqrt), reads=[bs[s]], writes=[bs[s]])
            P.dve(lambda e, s=s, r=r: e.reciprocal(out=rs[s][0:r], in_=rs[s][0:r]), reads=[bs[s]], writes=[bs[s]])
            P.act(lambda e, s=s, r=r: e.activation(out=XN[s][0:r], in_=XT[s][0:r], func=AF.Copy, scale=rs[s][0:r]),
                  reads=[bXT[s], bs[s]], writes=[bXN[s]])
            for kg in range(4):
                bk = 6 + (trk % 2); trk += 1
                tr3 = pbb[bk].rearrange("p (a b) -> p a b", b=128)
                for j in range(8):
                    kc = kg * 8 + j
                    P.pe(lambda e, s=s, r=r, kc=kc, j=j, tr3=tr3: e.transpose(out=tr3[:, j, 0:r], in_=XN[s][0:r, kc * 128:(kc + 1) * 128],
                                                                             identity=identb[0:r, 0:r]),
                         reads=[bXN[s]], writes=[bpb[bk]])
                P.dve(lambda e, r=r, kg=kg, col0=col0, tr3=tr3: e.tensor_tensor(
                    out=bigA[:, kg * 8:(kg + 1) * 8, col0:col0 + r], in0=tr3[:, :, 0:r],
                    in1=gvec[:, kg * 8:(kg + 1) * 8].unsqueeze(2).to_broadcast([128, 8, r]), op=ALU.mult),
                    reads=[bpb[bk]], writes=[bA])
        P.barrier()
        A.release(m)

    def phase_out(wout, tiles, dstbuf):
        m = A.mark()
        WO = [A.alloc([KC, 512], BF16) for _ in range(2)]; bWO = [Buf() for _ in range(2)]
        XR = [A.alloc([512], F32) for _ in range(3)]; bXR = [Buf() for _ in range(3)]
        XO = [A.alloc([512], F32) for _ in range(3)]; bXO = [Buf() for _ in range(3)]
        for kc in range(KC):
            P.dma("sp", bigA[:, kc, :], ogT[kc], writes=[bA])
        it = 0
        P.dma("pool", WO[0].rearrange("p a b -> p (a b)"), wout[0], writes=[bWO[0]])
        for cg in range(8):
            s = cg % 2
            if cg + 1 < 8:
                P.dma("pool", WO[1 - s].rearrange("p a b -> p (a b)"), wout[cg + 1], writes=[bWO[1 - s]])
            for (r, tcol0, xsrc, xdst, rb) in tiles:
                pj = it % 3; q = it % 3; it += 1
                P.dma("sp", XR[q][0:r], xsrc[:, cg * 512:(cg + 1) * 512], reads=rb, writes=[bXR[q]])
                for kc in range(KC):
                    P.pe(lambda e, pj=pj, r=r, kc=kc, tcol0=tcol0, s=s: e.matmul(
                        pb[pj][0:r, :], lhsT=bigA[:, kc, tcol0:tcol0 + r], rhs=WO[s][:, kc, :],
                        start=(kc == 0), stop=(kc == KC - 1)), reads=[bA, bWO[s]], writes=[bpb[pj]])
                P.dve(lambda e, pj=pj, q=q, r=r: e.tensor_tensor(out=XO[q][0:r], in0=pb[pj][0:r, :], in1=XR[q][0:r], op=ALU.add),
                      reads=[bpb[pj], bXR[q]], writes=[bXO[q]])
                P.dma("sp", xdst[:, cg * 512:(cg + 1) * 512], XO[q][0:r], reads=[bXO[q]])
        P.barrier()
        A.release(m)

    def phase_final(tiles):
        m = A.mark()
        FNB = A.alloc([D], F32); bF = Buf()
        P.dma("sp", FNB, fnorm_d.partition_broadcast(128), writes=[bF])
        XT = [A.alloc([D], F32) for _ in range(2)]; XY = [A.alloc([D], F32) for _ in range(2)]
        ss = [A.alloc([1], F32) for _ in range(2)]
        bXT = [Buf() for _ in range(2)]; bXY = [Buf() for _ in range(2)]; bs = [Buf() for _ in range(2)]
        for i, (src, r, dst) in enumerate(tiles):
            s = i % 2
            P.dma("sp", XT[s][0:r], src, reads=[b_x2], writes=[bXT[s]])
            P.act(lambda e, s=s, r=r: e.activation(out=XY[s][0:r], in_=XT[s][0:r], func=AF.Square, accum_out=ss[s][0:r]),
                  reads=[bXT[s]], writes=[bXY[s], bs[s]])
            P.dve(lambda e, s=s, r=r: e.tensor_scalar(out=ss[s][0:r], in0=ss[s][0:r], scalar1=1.0 / D, scalar2=EPS,
                                                    op0=ALU.mult, op1=ALU.add), reads=[bs[s]], writes=[bs[s]])
            P.act(lambda e, s=s, r=r: e.activation(out=ss[s][0:r], in_=ss[s][0:r], func=AF.Sqrt), reads=[bs[s]], writes=[bs[s]])
            P.dve(lambda e, s=s, r=r: e.reciprocal(out=ss[s][0:r], in_=ss[s][0:r]), reads=[bs[s]], writes=[bs[s]])
            P.dve(lambda e, s=s, r=r: e.scalar_tensor_tensor(out=XY[s][0:r], in0=XT[s][0:r], scalar=ss[s][0:r], in1=FNB[0:r],
                                                           op0=ALU.mult, op1=ALU.mult),
                  reads=[bXT[s], bs[s], bF], writes=[bXY[s]])
            P.dma("sp", dst, XY[s][0:r], reads=[bXY[s]], out=True)
        A.release(m)

    def phase_hgrn(prefix):
        m = A.mark()
        thirds = PTHIRDS if prefix else THIRDS
        H = DBG["heads"]
        NSLOT = 8
        WR = [A.alloc([KC, 128], BF16) for _ in range(NSLOT)]; bW = [Buf() for _ in range(NSLOT)]
        W = 388
        t_sg = A.alloc([W], F32); t_lf = A.alloc([W], F32); t_b = A.alloc([W], F32); t_d = A.alloc([W], F32)
        t_e1 = A.alloc([W], F32); t_e2 = A.alloc([W], F32)
        t_q = [A.alloc([W], F32) for _ in range(2)]; t_f = [A.alloc([W], F32) for _ in range(2)]
        t_k = [A.alloc([W], F32) for _ in range(2)]; t_gs = [A.alloc([W], F32) for _ in range(2)]
        QI = [A.alloc([384], BF16) for _ in range(2)]; KI = [A.alloc([384], BF16) for _ in range(2)]
        QS = [A.alloc([384], BF16) for _ in range(2)]; KK = [A.alloc([384], BF16) for _ in range(2)]
        VT = [A.alloc([384], BF16) for _ in range(2)]; EBL = [A.alloc([12], F32) for _ in range(2)]
        VS = [A.alloc([NS], F32) for _ in range(2)]
        t_sq = A.alloc([W], F32); t_rs = A.alloc([W], F32); t_t1 = A.alloc([W], F32); OG = A.alloc([W], BF16)
        KKtok = A.alloc([3, 128], BF16); Vtok = A.alloc([3, 128], BF16); AM = A.alloc([128], BF16)
        S = [A.alloc([128], F32) for _ in range(2)]; Sbf = A.alloc([128], BF16)
        SS = [A.alloc([128], F32) for _ in range(2)]; DG = A.alloc([128], F32); TMP = A.alloc([128], F32)
        SN = [A.alloc([128], F32) for _ in range(2)]
        b = {k: Buf(k) for k in ("sg", "lf", "b", "d", "e1", "e2", "sq", "rs", "t1", "OG", "KKtok", "Vtok", "AM", "Sbf", "DG", "TMP")}
        b2 = {k: [Buf(k + "0"), Buf(k + "1")] for k in ("q", "f", "k", "gs", "QI", "KI", "QS", "KK", "VT", "EBL", "VS", "S")}
        bSS = [Buf() for _ in range(2)]; bSN = [Buf() for _ in range(2)]
        groups = (1, 2) if prefix else (0, 1, 2, 3)
        pjc = [0]
        items = [(h, t3) for h in range(H) for t3 in range(3)]

        def wslot(h, gi):
            return (h % 2) * 4 + gi

        def load_w(h):
            for gi in groups:
                sl = wslot(h, gi)
                P.dma("pool", WR[sl].rearrange("p a b -> p (a b)"), w0in[gi * 32 + h], writes=[bW[sl]])

        def gen_A(i):
            h, t3 = items[i]
            c0, n, npr = thirds[t3]
            st = i % 2
            nch = npr // 32
            if t3 == 0 and h + 1 < H:
                load_w(h + 1)
            v3 = lambda t: t[:, 0:npr].rearrange("p (c t) -> p c t", t=32)
            for gi in groups:
                sl = wslot(h, gi)
                pj = pjc[0] % 3; pjc[0] += 1
                for kc in range(KC):
                    P.pe(lambda e, pj=pj, kc=kc, sl=sl: e.matmul(pb[pj][:, 0:n], lhsT=WR[sl][:, kc, :], rhs=bigA[:, kc, c0:c0 + n],
                                                                start=(kc == 0), stop=(kc == KC - 1)),
                         reads=[bW[sl], bA], writes=[bpb[pj]])
                    yield
                pp = pb[pj]; bp = bpb[pj]
                if gi == 0:
                    P.act(lambda e, pp=pp: e.activation(out=t_sg[:, 0:n], in_=pp[:, 0:n], func=AF.Sigmoid), reads=[bp], writes=[b["sg"]])
                    P.dve(lambda e, pp=pp: e.tensor_tensor(out=t_q[st][:, 0:n], in0=pp[:, 0:n], in1=t_sg[:, 0:n], op=ALU.mult),
                          reads=[bp, b["sg"]], writes=[b2["q"][st]])
                elif gi == 1:
                    P.act(lambda e, pp=pp: e.activation(out=t_sg[:, 0:n], in_=pp[:, 0:n], func=AF.Sigmoid), reads=[bp], writes=[b["sg"]])
                    P.dve(lambda e: e.tensor_scalar(out=t_f[st][:, 0:n], in0=t_sg[:, 0:n], scalar1=oml[:, h:h + 1], scalar2=lb[:, h:h + 1],
                                                    op0=ALU.mult, op1=ALU.add), reads=[b["sg"]], writes=[b2["f"][st]])
                    P.act(lambda e: e.activation(out=t_lf[:, 0:n], in_=t_f[st][:, 0:n], func=AF.Ln), reads=[b2["f"][st]], writes=[b["lf"]])
                    P.pool(lambda e: e.tensor_scalar(out=t_k[st][:, 0:n], in0=t_f[st][:, 0:n], scalar1=-1.0, scalar2=1.0,
                                                     op0=ALU.mult, op1=ALU.add), reads=[b2["f"][st]], writes=[b2["k"][st]])
                elif gi == 2:
                    P.act(lambda e, pp=pp: e.activation(out=VT[st][:, 0:npr], in_=pp[:, 0:npr], func=AF.Copy), reads=[bp], writes=[b2["VT"][st]])
                    if n > npr:
                        P.dve(lambda e, pp=pp: e.tensor_copy(out=VS[st], in_=pp[:, npr:n]), reads=[bp], writes=[b2["VS"][st]])
                else:
                    P.act(lambda e, pp=pp: e.activation(out=t_sg[:, 0:n], in_=pp[:, 0:n], func=AF.Sigmoid), reads=[bp], writes=[b["sg"]])
                    P.dve(lambda e, pp=pp: e.tensor_tensor(out=t_gs[st][:, 0:n], in0=pp[:, 0:n], in1=t_sg[:, 0:n], op=ALU.mult),
                          reads=[bp, b["sg"]], writes=[b2["gs"][st]])
            P.dve(lambda e: e.tensor_tensor_scan(out=t_b[:, 0:npr], data0=rmask[:, 0:npr], data1=t_lf[:, 0:npr], initial=0.0,
                                                 op0=ALU.mult, op1=ALU.add), reads=[b["lf"]], writes=[b["b"]])
            b3 = v3(t_b)
            if not prefix:
                P.dve(lambda e: e.tensor_tensor(out=v3(t_d), in0=b3, in1=b3[:, :, 15:16].to_broadcast([128, nch, 32]), op=ALU.subtract),
                      reads=[b["b"]], writes=[b["d"]])
                P.act(lambda e: e.activation(out=t_e1[:, 0:npr], in_=t_d[:, 0:npr], func=AF.Exp), reads=[b["d"]], writes=[b["e1"]])
                P.dve(lambda e: e.tensor_tensor(out=QI[st][:, 0:npr], in0=t_q[st][:, 0:npr], in1=t_e1[:, 0:npr], op=ALU.mult),
                      reads=[b2["q"][st], b["e1"]], writes=[b2["QI"][st]])
                P.act(lambda e: e.activation(out=t_e2[:, 0:npr], in_=t_d[:, 0:npr], func=AF.Exp, scale=-1.0), reads=[b["d"]], writes=[b["e2"]])
                P.pool(lambda e: e.tensor_tensor(out=KI[st][:, 0:npr], in0=t_k[st][:, 0:npr], in1=t_e2[:, 0:npr], op=ALU.mult),
                       reads=[b2["k"][st], b["e2"]], writes=[b2["KI"][st]])
                P.act(lambda e: e.activation(out=t_e1[:, 0:npr], in_=t_b[:, 0:npr], func=AF.Exp), reads=[b["b"]], writes=[b["e1"]])
                P.dve(lambda e: e.tensor_tensor(out=QS[st][:, 0:npr], in0=t_q[st][:, 0:npr], in1=t_e1[:, 0:npr], op=ALU.mult),
                      reads=[b2["q"][st], b["e1"]], writes=[b2["QS"][st]])
            P.pool(lambda e: e.tensor_tensor(out=v3(t_d), in0=b3[:, :, 31:32].to_broadcast([128, nch, 32]), in1=b3, op=ALU.subtract),
                   reads=[b["b"]], writes=[b["d"]])
            P.act(lambda e: e.activation(out=t_e2[:, 0:npr], in_=t_d[:, 0:npr], func=AF.Exp), reads=[b["d"]], writes=[b["e2"]])
            P.pool(lambda e: e.tensor_tensor(out=KK[st][:, 0:npr], in0=t_k[st][:, 0:npr], in1=t_e2[:, 0:npr], op=ALU.mult),
                   reads=[b2["k"][st], b["e2"]], writes=[b2["KK"][st]])
            P.act(lambda e: e.activation(out=EBL[st][:, 0:nch].unsqueeze(2), in_=b3[:, :, 31:32], func=AF.Exp),
                  reads=[b["b"]], writes=[b2["EBL"][st]])

        def advance(g, k):
            if g is None:
                return
            for _ in range(k):
                try:
                    next(g)
                except StopIteration:
                    return

        def epilogue(ot, n, gs_ap, bgs, og_ap, h, ms_ap, bms, bot):
            P.act(lambda e: e.activation(out=t_sq[:, 0:n], in_=ot, func=AF.Square), reads=[bot], writes=[b["sq"]])
            P.pe(lambda e: e.matmul(ms_ap, lhsT=onesm, rhs=t_sq[:, 0:n], start=True, stop=True), reads=[b["sq"]], writes=[bms])
            P.act(lambda e: e.activation(out=t_rs[:, 0:n], in_=ms_ap, func=AF.Sqrt, bias=EPS, scale=1.0), reads=[bms], writes=[b["rs"]])
            P.dve(lambda e: e.reciprocal(out=t_rs[:, 0:n], in_=t_rs[:, 0:n]), reads=[b["rs"]], writes=[b["rs"]])
            P.dve(lambda e: e.tensor_tensor(out=t_t1[:, 0:n], in0=ot, in1=t_rs[:, 0:n], op=ALU.mult), reads=[bot, b["rs"]], writes=[b["t1"]])
            P.dve(lambda e: e.scalar_tensor_tensor(out=og_ap, in0=t_t1[:, 0:n], scalar=g_on[:, h:h + 1], in1=gs_ap, op0=ALU.mult,
                                                   op1=ALU.mult), reads=[b["t1"], bgs], writes=[b["OG"]])

        def stage_B(i, g):
            h, t3 = items[i]
            c0, n, npr = thirds[t3]
            st = i % 2
            nblk = npr // 128
            Sh = S[h % 2]; bS = b2["S"][h % 2]
            nmm = len(groups) * KC
            if t3 == 0:
                if prefix:
                    P.dve(lambda e: e.memset(Sh, 0.0), writes=[bS])
                else:
                    P.dma("sp", Sh, spre[h], reads=[b_spre[h]], writes=[bS])
                    P.act(lambda e: e.activation(out=Sbf, in_=Sh, func=AF.Copy), reads=[bS], writes=[b["Sbf"]])
            advance(g, nmm // 3)
            per_chunk = -(-(nmm - nmm // 3) // (4 * nblk))
            tr3 = pbb[6].rearrange("p (a b) -> p a b", b=128)
            for blk in range(nblk):
                P.pe(lambda e, blk=blk: e.transpose(out=tr3[:, blk, :], in_=KK[st][:, blk * 128:(blk + 1) * 128], identity=identb),
                     reads=[b2["KK"][st]], writes=[bpb[6]])
                P.pe(lambda e, blk=blk: e.transpose(out=tr3[:, 3 + blk, :], in_=VT[st][:, blk * 128:(blk + 1) * 128], identity=identb),
                     reads=[b2["VT"][st]], writes=[bpb[6]])
            P.act(lambda e: e.activation(out=KKtok[:, 0:nblk, :], in_=tr3[:, 0:nblk, :], func=AF.Copy), reads=[bpb[6]], writes=[b["KKtok"]])
            P.dve(lambda e: e.tensor_copy(out=Vtok[:, 0:nblk, :], in_=tr3[:, 3:3 + nblk, :]), reads=[bpb[6]], writes=[b["Vtok"]])
            sui = 0
            for blk in range(nblk):
                if not prefix:
                    P.pe(lambda e, blk=blk: e.matmul(pb[3][:, 0:128], lhsT=KI[st][:, blk * 128:(blk + 1) * 128],
                                                     rhs=QI[st][:, blk * 128:(blk + 1) * 128], start=True, stop=True),
                         reads=[b2["KI"][st], b2["QI"][st]], writes=[bpb[3]])
                    P.dve(lambda e: e.tensor_tensor(out=AM, in0=pb[3][:, 0:128], in1=cmask, op=ALU.mult), reads=[bpb[3]], writes=[b["AM"]])
                for c in range(4):
                    ch = blk * 4 + c
                    cs = blk * 128 + c * 32
                    sub = (5, 3)[sui % 2] if prefix else 5
                    sui += 1
                    kw = {"tile_position": (96, 0)} if c == 3 else {}
                    P.pe(lambda e, c=c, blk=blk, kw=kw, sub=sub: e.matmul(pb[sub][:, 0:128], lhsT=KKtok[32 * c:32 * c + 32, blk, :],
                                                                         rhs=Vtok[32 * c:32 * c + 32, blk, :], start=True, stop=True, **kw),
                         reads=[b["KKtok"], b["Vtok"]], writes=[bpb[sub]])
                    if not prefix:
                        P.pe(lambda e, cs=cs: e.matmul(pb[4][:, cs:cs + 32], lhsT=Sbf, rhs=QS[st][:, cs:cs + 32], start=True, stop=False),
                             reads=[b["Sbf"], b2["QS"][st]], writes=[bpb[4]])
                        P.pe(lambda e, cs=cs, c=c, blk=blk: e.matmul(pb[4][:, cs:cs + 32], lhsT=Vtok[:, blk, :], rhs=AM[:, c * 32:(c + 1) * 32],
                                                                    start=False, stop=True), reads=[b["Vtok"], b["AM"]], writes=[bpb[4]])
                    P.dve(lambda e, ch=ch, sub=sub: e.scalar_tensor_tensor(out=Sh, in0=Sh, scalar=EBL[st][:, ch:ch + 1], in1=pb[sub][:, 0:128],
                                                                          op0=ALU.mult, op1=ALU.add),
                          reads=[bS, b2["EBL"][st], bpb[sub]], writes=[bS])
                    if not prefix:
                        P.act(lambda e: e.activation(out=Sbf, in_=Sh, func=AF.Copy), reads=[bS], writes=[b["Sbf"]])
                    advance(g, per_chunk)
            if not prefix:
                epilogue(pb[4][:, 0:npr], npr, t_gs[st][:, 0:npr], b2["gs"][st], OG[:, 0:npr], h, pb[7][:, 0:npr], bpb[7], bpb[4])
                P.dma("sp", ogT[h, :, c0:c0 + npr], OG[:, 0:npr], reads=[b["OG"]])
                if n > npr:
                    for bi in range(NS):
                        col = npr + bi
                        s2 = bi % 2
                        P.dma("sp", SS[s2], st_in[bi, h], writes=[bSS[s2]])
                        P.pool(lambda e, bi=bi: e.tensor_scalar(out=DG, in0=identf, scalar1=VS[st][:, bi:bi + 1], scalar2=None, op0=ALU.mult),
                               reads=[b2["VS"][st]], writes=[b["DG"]])
                        P.pe(lambda e: e.matmul(pb[3][:, 128:256], lhsT=ones1, rhs=DG, start=True, stop=True), reads=[b["DG"]], writes=[bpb[3]])
                        P.pool(lambda e, s2=s2, col=col: e.tensor_scalar(out=TMP, in0=SS[s2], scalar1=t_f[st][:, col:col + 1], scalar2=None,
                                                                        op0=ALU.mult), reads=[bSS[s2], b2["f"][st]], writes=[b["TMP"]])
                        P.dve(lambda e, s2=s2, col=col: e.scalar_tensor_tensor(out=SN[s2], in0=pb[3][:, 128:256], scalar=t_k[st][:, col:col + 1],
                                                                              in1=TMP, op0=ALU.mult, op1=ALU.add),
                              reads=[bpb[3], b2["k"][st], b["TMP"]], writes=[bSN[s2]])
                        P.pe(lambda e, s2=s2, col=col, bi=bi: e.matmul(pb[5][:, 256 + bi:257 + bi], lhsT=SN[s2], rhs=t_q[st][:, col:col + 1],
                                                                      start=True, stop=True), reads=[bSN[s2], b2["q"][st]], writes=[bpb[5]])
                        P.dma("sp", sts_o[bi, h], SN[s2], reads=[bSN[s2]], out=True)
                    epilogue(pb[5][:, 256:256 + NS], NS, t_gs[st][:, npr:n], b2["gs"][st], OG[:, npr:n], h, pb[7][:, 384:384 + NS],
                             bpb[7], bpb[5])
                    P.dma("sp", ogT[h, :, NT:NTOK], OG[:, npr:n], reads=[b["OG"]])
            if t3 == 2:
                if prefix:
                    P.dma("sp", spre[h], Sh, reads=[bS], writes=[b_spre[h]])
                else:
                    P.dma("sp", stp_o[h], Sh, reads=[bS], out=True)
            advance(g, 10 ** 6)

        load_w(0)
        g0 = gen_A(0)
        advance(g0, 10 ** 6)
        for i in range(len(items)):
            g = gen_A(i + 1) if i + 1 < len(items) else None
            stage_B(i, g)
        P.barrier()
        A.release(m)

    def phase_swa():
        m = A.mark()
        QsT = A.alloc([32, NS], F32); KsT = A.alloc([8, NS], F32); VsT = A.alloc([4, NS], F32); GSs = A.alloc([32, NS], F32)
        EXPU = A.alloc([384], F32, parts=64); KPtok = A.alloc([512], F32); VPtok = A.alloc([512], F32)
        m2 = A.mark()
        NSLOT = 6
        WR = [A.alloc([KC, 128], BF16) for _ in range(NSLOT)]; bW = [Buf() for _ in range(NSLOT)]
        QTn = A.alloc([4, NTOK], BF16); KTn = A.alloc([NTOK], BF16); VTt = A.alloc([384], BF16)
        Vtok = A.alloc([9, 128], BF16); GSn = A.alloc([4, NTOK], BF16); OGn = QTn
        On = A.alloc([8, 512], BF16); t_sg = A.alloc([388], F32)
        KL = A.alloc([128], F32); VL = A.alloc([128], F32)
        EBn = A.alloc([8, 256], BF16)
        relb_sb = A.alloc([64], F32, parts=32); oneh_sb = A.alloc([384], F32, parts=32); win_sb = A.alloc([384], F32, parts=64)
        Xs = [A.alloc([256], F32) for _ in range(2)]
        Pf = [A.alloc([256], F32) for _ in range(2)]; PB = [A.alloc([256], BF16) for _ in range(2)]
        PT = [A.alloc([2, 128], BF16) for _ in range(2)]
        mx = [A.alloc([1], F32) for _ in range(2)]; nb = [A.alloc([1], F32) for _ in range(2)]
        rs = [A.alloc([1], F32) for _ in range(2)]; esk = [A.alloc([1], F32) for _ in range(2)]
        den = [A.alloc([1], F32) for _ in range(2)]
        b = {k: Buf(k) for k in ("QTn", "KTn", "VTt", "Vtok", "GSn", "OGn", "On", "sg", "KL", "VL", "KPtok", "VPtok", "QsT",
                                 "KsT", "VsT", "GSs", "EBIAS", "EXPU", "setup")}
        bX = [Buf() for _ in range(2)]; bPf = [Buf() for _ in range(2)]; bPB = [Buf() for _ in range(2)]
        bPT = [Buf() for _ in range(2)]; bst = [Buf() for _ in range(2)]
        tr6 = pbb[6].rearrange("p (a b) -> p a b", b=128)
        tr7 = pbb[7].rearrange("p (a b) -> p a b", b=128)

        P.dma("sp", relb_sb, relb_d, writes=[b["setup"]])
        P.dma("sp", oneh_sb, c_onehot, writes=[Buf()])
        P.dma("sp", win_sb, c_win, writes=[Buf()])
        P.barrier()
        P.pe(lambda e: e.matmul(pb[3][0:64, 0:384], lhsT=relb_sb, rhs=oneh_sb, start=True, stop=True), writes=[bpb[3]])
        P.act(lambda e: e.activation(out=EXPU, in_=pb[3][0:64, 0:384], func=AF.Exp), reads=[bpb[3]], writes=[b["EXPU"]])
        P.dve(lambda e: e.tensor_tensor(out=EXPU, in0=EXPU, in1=win_sb, op=ALU.mult), reads=[b["EXPU"]], writes=[b["EXPU"]])
        P.dma("sp", ub, EXPU, reads=[b["EXPU"]], writes=[b_ub])
        def build_eb(n):
            for g in range(8):
                h = 8 * n + g
                s = g % 2
                P.dma("sp", Xs[s], bass.AP(ub.tensor, h * 384, [[1, 128], [1, 256]]), reads=[b_ub], writes=[bX[s]])
                P.pe(lambda e, s=s: e.matmul(pb[3 + s][:, 0:256], lhsT=anti, rhs=Xs[s], start=True, stop=True),
                     reads=[bX[s]], writes=[bpb[3 + s]])
                if s == 0:
                    P.act(lambda e, g=g, s=s: e.activation(out=EBn[:, g, :], in_=pb[3 + s][:, 0:256], func=AF.Copy),
                          reads=[bpb[3 + s]], writes=[b["EBIAS"]])
                else:
                    P.dve(lambda e, g=g, s=s: e.tensor_copy(out=EBn[:, g, :], in_=pb[3 + s][:, 0:256]),
                          reads=[bpb[3 + s]], writes=[b["EBIAS"]])

        order = []
        for n in range(8):
            order += [("q", n, gi) for gi in range(4)] + [("k", n, 0)]
            if n % 2 == 0:
                order += [("v", n, 0)]
            order += [("g", n, gi) for gi in range(4)]
        AHEAD = 4

        def load_w(pos):
            sl = pos % NSLOT
            P.dma("pool", WR[sl].rearrange("p a b -> p (a b)"), w1in[pos], writes=[bW[sl]])

        for pos in range(AHEAD):
            load_w(pos)
        pjc = [0]

        def do_group(pos, kind, n, gi):
            sl = pos % NSLOT
            for t3, (c0, nn, npr) in enumerate(THIRDS):
                pj = pjc[0] % 3; pjc[0] += 1
                for kc in range(KC):
                    P.pe(lambda e, pj=pj, kc=kc, c0=c0, nn=nn: e.matmul(pb[pj][:, 0:nn], lhsT=WR[sl][:, kc, :],
                                                                        rhs=bigA[:, kc, c0:c0 + nn], start=(kc == 0),
                                                                        stop=(kc == KC - 1)),
                         reads=[bW[sl], bA], writes=[bpb[pj]])
                pp = pb[pj]; bp = bpb[pj]
                last = nn > npr
                if kind == "q":
                    P.act(lambda e, pp=pp, c0=c0, nn=nn: e.activation(out=QTn[:, gi, c0:c0 + nn], in_=pp[:, 0:nn], func=AF.Copy),
                          reads=[bp], writes=[b["QTn"]])
                    if last:
                        P.dve(lambda e, pp=pp, npr=npr, nn=nn: e.tensor_copy(out=QsT[:, 4 * n + gi, :], in_=pp[:, npr:nn]),
                              reads=[bp], writes=[b["QsT"]])
                elif kind == "k":
                    P.act(lambda e, pp=pp, c0=c0, nn=nn: e.activation(out=KTn[:, c0:c0 + nn], in_=pp[:, 0:nn], func=AF.Copy),
                          reads=[bp], writes=[b["KTn"]])
                    if last:
                        P.dve(lambda e, pp=pp: e.tensor_copy(out=KL, in_=pp[:, 256:384]), reads=[bp], writes=[b["KL"]])
                        P.dve(lambda e, pp=pp, npr=npr, nn=nn: e.tensor_copy(out=KsT[:, n, :], in_=pp[:, npr:nn]),
                              reads=[bp], writes=[b["KsT"]])
                        P.pe(lambda e: e.transpose(out=pb[7][:, 256:384], in_=KL, identity=identf), reads=[b["KL"]], writes=[bpb[7]])
                        P.dve(lambda e: e.tensor_copy(out=KPtok[:, n * 64:(n + 1) * 64], in_=pb[7][:, 256:320]),
                              reads=[bpb[7]], writes=[b["KPtok"]])
                elif kind == "v":
                    mm_ = n // 2
                    P.act(lambda e, pp=pp, npr=npr: e.activation(out=VTt[:, 0:npr], in_=pp[:, 0:npr], func=AF.Copy),
                          reads=[bp], writes=[b["VTt"]])
                    for blk in range(3):
                        P.pe(lambda e, blk=blk: e.transpose(out=tr6[:, 4 + blk, :], in_=VTt[:, blk * 128:(blk + 1) * 128], identity=identb),
                             reads=[b["VTt"]], writes=[bpb[6]])
                    P.dve(lambda e, t3=t3: e.tensor_copy(out=Vtok[:, 3 * t3:3 * t3 + 3, :], in_=tr6[:, 4:7, :]),
                          reads=[bpb[6]], writes=[b["Vtok"]])
                    if last:
                        P.dve(lambda e, pp=pp: e.tensor_copy(out=VL, in_=pp[:, 256:384]), reads=[bp], writes=[b["VL"]])
                        P.dve(lambda e, pp=pp, npr=npr, nn=nn: e.tensor_copy(out=VsT[:, mm_, :], in_=pp[:, npr:nn]),
                              reads=[bp], writes=[b["VsT"]])
                        P.pe(lambda e: e.transpose(out=pb[7][:, 256:384], in_=VL, identity=identf), reads=[b["VL"]], writes=[bpb[7]])
                        P.dve(lambda e: e.tensor_copy(out=VPtok[:, mm_ * 128:(mm_ + 1) * 128], in_=pb[7][:, 256:384]),
                              reads=[bpb[7]], writes=[b["VPtok"]])
                else:
                    P.act(lambda e, pp=pp, nn=nn: e.activation(out=t_sg[:, 0:nn], in_=pp[:, 0:nn], func=AF.Sigmoid),
                          reads=[bp], writes=[b["sg"]])
                    P.dve(lambda e, pp=pp, c0=c0, nn=nn: e.tensor_tensor(out=GSn[:, gi, c0:c0 + nn], in0=pp[:, 0:nn], in1=t_sg[:, 0:nn],
                                                                         op=ALU.mult), reads=[bp, b["sg"]], writes=[b["GSn"]])
                    if last:
                        P.dve(lambda e, pp=pp, npr=npr, nn=nn: e.tensor_tensor(out=GSs[:, 4 * n + gi, :], in0=pp[:, npr:nn],
                                                                               in1=t_sg[:, npr:nn], op=ALU.mult),
                              reads=[bp, b["sg"]], writes=[b["GSs"]])

        itc = [0]

        def attend(n, j, g):
            h = 8 * n + g; gi = g // 2; hb = 64 * (g % 2); vb = 64 * (n % 2)
            s = itc[0] % 2; itc[0] += 1
            ps = pb[3 + s]; bps = bpb[3 + s]
            P.pe(lambda e: e.matmul(ps[:, 0:256], lhsT=QTn[hb:hb + 64, gi, j * 128:(j + 1) * 128],
                                    rhs=KTn[hb:hb + 64, (j - 1) * 128:(j + 1) * 128], start=True, stop=True),
                 reads=[b["QTn"], b["KTn"]], writes=[bps])
            P.dve(lambda e: e.reduce_max(out=mx[s], in_=ps[:, 0:256], axis=AX.X), reads=[bps], writes=[bst[s]])
            P.dve(lambda e: e.tensor_scalar(out=nb[s], in0=mx[s], scalar1=-SCALE, scalar2=None, op0=ALU.mult),
                  reads=[bst[s]], writes=[bst[s]])
            P.act(lambda e: e.activation(out=Pf[s], in_=ps[:, 0:256], func=AF.Exp, bias=nb[s], scale=SCALE),
                  reads=[bps, bst[s]], writes=[bPf[s]])
            if j == 1:
                P.dve(lambda e: e.tensor_tensor(out=Pf[s], in0=Pf[s], in1=fmask, op=ALU.mult), reads=[bPf[s]], writes=[bPf[s]])
            P.dve(lambda e: e.scalar_tensor_tensor(out=PB[s], in0=Pf[s], scalar=1.0, in1=EBn[:, g, :], op0=ALU.mult, op1=ALU.mult,
                                                   accum_out=rs[s]), reads=[bPf[s], b["EBIAS"]], writes=[bPB[s], bst[s]])
            P.act(lambda e: e.activation(out=esk[s], in_=nb[s], func=AF.Exp, bias=sinkb[:, h:h + 1], scale=1.0),
                  reads=[bst[s]], writes=[bst[s]])
            P.dve(lambda e: e.tensor_tensor(out=den[s], in0=rs[s], in1=esk[s], op=ALU.add), reads=[bst[s]], writes=[bst[s]])
            P.dve(lambda e: e.reciprocal(out=den[s], in_=den[s]), reads=[bst[s]], writes=[bst[s]])
            for hf in range(2):
                P.pe(lambda e, hf=hf: e.transpose(out=tr6[:, 2 * s + hf, :], in_=PB[s][:, hf * 128:(hf + 1) * 128], identity=identb),
                     reads=[bPB[s]], writes=[bpb[6]])
            P.act(lambda e: e.activation(out=PT[s], in_=tr6[:, 2 * s:2 * s + 2, :], func=AF.Copy), reads=[bpb[6]], writes=[bPT[s]])
            P.pe(lambda e: e.matmul(pb[5][:, s * 64:(s + 1) * 64], lhsT=PT[s][:, 0, :], rhs=Vtok[:, j - 1, vb:vb + 64],
                                    start=True, stop=False), reads=[bPT[s], b["Vtok"]], writes=[bpb[5]])
            P.pe(lambda e: e.matmul(pb[5][:, s * 64:(s + 1) * 64], lhsT=PT[s][:, 1, :], rhs=Vtok[:, j, vb:vb + 64],
                                    start=False, stop=True), reads=[bPT[s], b["Vtok"]], writes=[bpb[5]])
            P.act(lambda e: e.activation(out=On[:, j - 1, g * 64:(g + 1) * 64], in_=pb[5][:, s * 64:(s + 1) * 64], func=AF.Copy,
                                         scale=den[s]), reads=[bpb[5], bst[s]], writes=[b["On"]])

        def finish_n(n):
            for j in range(1, 9):
                for gi in range(4):
                    P.pe(lambda e, j=j, gi=gi: e.transpose(out=tr7[:, gi, :], in_=On[:, j - 1, gi * 128:(gi + 1) * 128], identity=identb),
                         reads=[b["On"]], writes=[bpb[7]])
                P.dve(lambda e, j=j: e.tensor_tensor(out=OGn[:, :, j * 128:(j + 1) * 128], in0=tr7[:, 0:4, :],
                                                     in1=GSn[:, :, j * 128:(j + 1) * 128], op=ALU.mult),
                      reads=[bpb[7], b["GSn"]], writes=[b["QTn"]])
            for gi in range(4):
                P.dma("sp", ogT[4 * n + gi, :, 128:NT], OGn[:, gi, 128:NT], reads=[b["QTn"]])

        pos = 0
        for n in range(8):
            while pos < len(order) and order[pos][1] == n:
                kind, nn_, gi = order[pos]
                do_group(pos, kind, nn_, gi)
                if pos + AHEAD < len(order):
                    load_w(pos + AHEAD)
                pos += 1
            build_eb(n)
            for j in range(1, 9):
                for g in range(8):
                    attend(n, j, g)
            finish_n(n)
        P.dma("sp", kp_o, KPtok, reads=[b["KPtok"]], out=True)
        P.dma("sp", vp_o, VPtok, reads=[b["VPtok"]], out=True)
        P.barrier()
        A.release(m2)
        keep = (QsT, KsT, VsT, GSs, EXPU)
        swa_samples(keep)
        P.barrier()
        A.release(m)

    def swa_samples(keep):
        QsT, KsT, VsT, GSs, EXPU = keep
        QS_tok = A.alloc([D], F32); KN_tok = A.alloc([512], F32); VN_tok = A.alloc([512], F32)
        Kc = A.alloc([512], F32); Vc = A.alloc([512], F32); QB = A.alloc([D], F32); PROD = A.alloc([D], F32)
        knr = A.alloc([512], F32); vnr = A.alloc([512], F32)
        SCT = A.alloc([64], F32); scn = A.alloc([64], F32); PS = A.alloc([129], F32); PBs = A.alloc([129], F32)
        PTs = A.alloc([64], F32); pn = A.alloc([64], F32); SEL = A.alloc([512], F32); OS = A.alloc([64], F32)
        mxs = A.alloc([1], F32); nbs = A.alloc([1], F32); rss = A.alloc([1], F32); ess = A.alloc([1], F32); dens = A.alloc([1], F32)
        O_s = A.alloc([D], F32); OGS = A.alloc([32, NS], BF16)
        bb = {k: Buf(k) for k in ("QS_tok", "KN_tok", "VN_tok", "Kc", "Vc", "QB", "PROD", "knr", "vnr", "SCT", "scn", "PS", "PBs",
                                  "PTs", "pn", "SEL", "OS", "st", "O_s", "OGS")}
        for grp in range(32):
            bk = 3 + (grp // 4) % 2
            P.pe(lambda e, grp=grp, bk=bk: e.transpose(out=pb[bk][0:NS, (grp % 4) * 128:(grp % 4 + 1) * 128], in_=QsT[:, grp, :],
                                                       identity=identf), writes=[bpb[bk]])
            if grp % 4 == 3:
                P.dve(lambda e, grp=grp, bk=bk: e.tensor_copy(out=QS_tok[0:NS, (grp - 3) * 128:(grp + 1) * 128], in_=pb[bk][0:NS, :]),
                      reads=[bpb[bk]], writes=[bb["QS_tok"]])
        for n in range(8):
            P.pe(lambda e, n=n: e.transpose(out=pb[5][0:NS, n * 64:(n + 1) * 64], in_=KsT[0:64, n, :], identity=identf[0:64, 0:64]),
                 writes=[bpb[5]])
        P.dve(lambda e: e.tensor_copy(out=KN_tok[0:NS, :], in_=pb[5][0:NS, :]), reads=[bpb[5]], writes=[bb["KN_tok"]])
        for mi in range(4):
            P.pe(lambda e, mi=mi: e.transpose(out=pb[5][0:NS, mi * 128:(mi + 1) * 128], in_=VsT[:, mi, :], identity=identf),
                 reads=[bb["KN_tok"]], writes=[bpb[5]])
        P.dve(lambda e: e.tensor_copy(out=VN_tok[0:NS, :], in_=pb[5][0:NS, :]), reads=[bpb[5]], writes=[bb["VN_tok"]])
        P.dma("sp", qs_s, QS_tok[0:NS, :], reads=[bb["QS_tok"]], writes=[b_qs])
        P.dma("sp", kn_s, KN_tok[0:NS, :], reads=[bb["KN_tok"]], writes=[b_kn])
        P.dma("sp", vn_s, VN_tok[0:NS, :], reads=[bb["VN_tok"]], writes=[b_vn])
        P.dma("sp", ks_o[:, 127, :], KN_tok[0:NS, :], reads=[bb["KN_tok"]], out=True)
        P.dma("sp", vs_o[:, 127, :], VN_tok[0:NS, :], reads=[bb["VN_tok"]], out=True)
        for bi in range(NS):
            P.dma("sp", ks_o[bi, 0:127, :], ck[bi, 1:128, :], out=True)
            P.dma("sp", vs_o[bi, 0:127, :], cv[bi, 1:128, :], out=True)
        for bi in range(NS):
            P.dma("sp", Kc, ck[bi], writes=[bb["Kc"]])
            P.dma("sp", Vc, cv[bi], writes=[bb["Vc"]])
            P.dma("sp", QB, qs_s[bi:bi + 1, :].partition_broadcast(128), reads=[b_qs], writes=[bb["QB"]])
            P.dma("sp", knr[0:1, :], kn_s[bi:bi + 1, :], reads=[b_kn], writes=[bb["knr"]])
            P.dma("sp", vnr[0:1, :], vn_s[bi:bi + 1, :], reads=[b_vn], writes=[bb["vnr"]])
            q4 = lambda t, r: t[0:r, :].rearrange("p (n g d) -> p n g d", n=8, g=8)
            k4 = lambda t, r: t[0:r, :].rearrange("p (n d) -> p n d", n=8).unsqueeze(2).to_broadcast([r, 8, 8, 64])
            P.dve(lambda e: e.tensor_tensor(out=q4(PROD, 128), in0=q4(QB, 128), in1=k4(Kc, 128), op=ALU.mult),
                  reads=[bb["QB"], bb["Kc"]], writes=[bb["PROD"]])
            P.dve(lambda e: e.tensor_reduce(out=SCT, in_=PROD.rearrange("p (h d) -> p h d", d=64), axis=AX.X, op=ALU.add),
                  reads=[bb["PROD"]], writes=[bb["SCT"]])
            P.dve(lambda e: e.tensor_tensor(out=q4(PROD, 1), in0=q4(QB, 1), in1=k4(knr, 1), op=ALU.mult),
                  reads=[bb["QB"], bb["knr"], bb["SCT"]], writes=[bb["PROD"]])
            P.dve(lambda e: e.tensor_reduce(out=scn[0:1, :], in_=PROD[0:1, :].rearrange("p (h d) -> p h d", d=64), axis=AX.X, op=ALU.add),
                  reads=[bb["PROD"]], writes=[bb["scn"]])
            P.pe(lambda e: e.transpose(out=pb[3][0:64, 0:128], in_=SCT, identity=identf), reads=[bb["SCT"]], writes=[bpb[3]])
            P.pe(lambda e: e.transpose(out=pb[3][0:64, 128:129], in_=scn[0:1, :], identity=identf[0:1, 0:1]),
                 reads=[bb["scn"]], writes=[bpb[3]])
            P.dve(lambda e: e.reduce_max(out=mxs[0:64], in_=pb[3][0:64, 0:129], axis=AX.X), reads=[bpb[3]], writes=[bb["st"]])
            P.dve(lambda e: e.tensor_scalar(out=nbs[0:64], in0=mxs[0:64], scalar1=-SCALE, scalar2=None, op0=ALU.mult),
                  reads=[bb["st"]], writes=[bb["st"]])
            P.act(lambda e: e.activation(out=PS[0:64], in_=pb[3][0:64, 0:129], func=AF.Exp, bias=nbs[0:64], scale=SCALE),
                  reads=[bpb[3], bb["st"]], writes=[bb["PS"]])
            P.dve(lambda e: e.scalar_tensor_tensor(out=PBs[0:64], in0=PS[0:64], scalar=1.0, in1=EXPU[:, 127:256], op0=ALU.mult,
                                                   op1=ALU.mult, accum_out=rss[0:64]), reads=[bb["PS"]], writes=[bb["PBs"], bb["st"]])
            P.act(lambda e: e.activation(out=ess[0:64], in_=nbs[0:64], func=AF.Exp, bias=sinkc, scale=1.0),
                  reads=[bb["st"]], writes=[bb["st"]])
            P.dve(lambda e: e.tensor_tensor(out=dens[0:64], in0=rss[0:64], in1=ess[0:64], op=ALU.add), reads=[bb["st"]], writes=[bb["st"]])
            P.dve(lambda e: e.reciprocal(out=dens[0:64], in_=dens[0:64]), reads=[bb["st"]], writes=[bb["st"]])
            P.pe(lambda e: e.transpose(out=pb[4][:, 0:64], in_=PBs[0:64, 0:128], identity=identf[0:64, 0:64]),
                 reads=[bb["PBs"]], writes=[bpb[4]])
            P.pe(lambda e: e.transpose(out=pb[4][0:1, 64:128], in_=PBs[0:64, 128:129], identity=identf[0:64, 0:64]),
                 reads=[bb["PBs"]], writes=[bpb[4]])
            P.dve(lambda e: e.tensor_copy(out=PTs, in_=pb[4][:, 0:64]), reads=[bpb[4]], writes=[bb["PTs"]])
            P.dve(lambda e: e.tensor_copy(out=pn[0:1, :], in_=pb[4][0:1, 64:128]), reads=[bpb[4]], writes=[bb["pn"]])
            P.pe(lambda e: e.matmul(pb[5][0:64, :], lhsT=PTs, rhs=Vc, start=True, stop=False), reads=[bb["PTs"], bb["Vc"]], writes=[bpb[5]])
            P.pe(lambda e: e.matmul(pb[5][0:64, :], lhsT=pn[0:1, :], rhs=vnr[0:1, :], start=False, stop=True),
                 reads=[bb["pn"], bb["vnr"]], writes=[bpb[5]])
            P.dve(lambda e: e.tensor_tensor(out=SEL[0:64, :].rearrange("p (n d) -> p n d", n=8),
                                            in0=pb[5][0:64, :].rearrange("p (n d) -> p n d", n=8),
                                            in1=hsel.unsqueeze(2).to_broadcast([64, 8, 64]), op=ALU.mult),
                  reads=[bpb[5]], writes=[bb["SEL"]])
            P.dve(lambda e: e.tensor_reduce(out=OS[0:64], in_=SEL[0:64, :].rearrange("p (n d) -> p d n", n=8), axis=AX.X, op=ALU.add),
                  reads=[bb["SEL"]], writes=[bb["OS"]])
            P.dve(lambda e: e.tensor_scalar(out=OS[0:64], in0=OS[0:64], scalar1=dens[0:64], scalar2=None, op0=ALU.mult),
                  reads=[bb["OS"], bb["st"]], writes=[bb["OS"]])
            P.dma("sp", os_s[bi].rearrange("(h d) -> h d", d=64), OS[0:64], reads=[bb["OS"]], writes=[b_os])
        P.dma("sp", O_s[0:NS, :], os_s, reads=[b_os], writes=[bb["O_s"]])
        for grp in range(32):
            P.pe(lambda e, grp=grp: e.transpose(out=pb[7][:, grp * NS:(grp + 1) * NS], in_=O_s[0:NS, grp * 128:(grp + 1) * 128],
                                                identity=identf[0:NS, 0:NS]), reads=[bb["O_s"]], writes=[bpb[7]])
        P.dve(lambda e: e.tensor_tensor(out=OGS, in0=pb[7][:, 0:32 * NS].rearrange("p (g s) -> p g s", s=NS), in1=GSs, op=ALU.mult),
              reads=[bpb[7]], writes=[bb["OGS"]])
        P.dma("sp", ogT[:, :, NT:NTOK].rearrange("k p c -> p k c"), OGS, reads=[bb["OGS"]])

    rows = lambda t, k: t[k * 128:(k + 1) * 128, :]
    steps = [
        lambda: phase_norm([(rows(xpre, t), 128, t * 128, []) for t in range(8)], g_hg),
        lambda: phase_hgrn(True),
        lambda: phase_norm([(rows(xm, t), 128, t * 128, []) for t in range(9)] + [(xs, NS, NT, [])], g_hg),
        lambda: phase_hgrn(False),
        lambda: phase_out(w0out, [(128, t * 128, rows(xm, t), rows(x1, t), []) for t in range(9)]
                          + [(NS, NT, xs, x1[NT:NTOK, :], [])], b_x1),
        lambda: phase_norm([(rows(x1, t), 128, t * 128, []) for t in range(9)] + [(x1[NT:NTOK, :], NS, NT, [])], g_sw),
        lambda: phase_swa(),
        lambda: phase_out(w1out, [(128, t * 128, rows(x1, t), rows(x2, t), []) for t in range(1, 9)]
                          + [(NS, NT, x1[NT:NTOK, :], x2[NT:NTOK, :], [])], b_x2),
        lambda: phase_final([(rows(x2, t), 128, rows(y_o, t - 1)) for t in range(1, 9)] + [(x2[NT:NTOK, :], NS, ys_o)]),
    ]
    for i, st in enumerate(steps):
        if stop is not None and i >= stop:
            break
        st()
    cnt = P.build()
    es.close()
    return nc, cnt


def _t5_bucket(d):
    d = np.maximum(d, 0)
    df = np.maximum(d, 1).astype(np.float32)
    large = 16 + (np.log(df / np.float32(16)) / np.float32(np.log(128 / 16)) * np.float32(16)).astype(np.int32)
    large = np.minimum(large, 31)
    return np.where(d < 16, d, large)


def _consts(first_half):
    c = {}
    c["c_identf"] = np.eye(128, dtype=np.float32)
    c["c_anti"] = np.ascontiguousarray(np.eye(128, dtype=np.float32)[::-1])
    s = np.arange(128)[:, None]; t = np.arange(128)[None, :]
    c["c_cmask"] = ((s // 32 == t // 32) & (s <= t)).astype(np.float32)
    rm = np.ones((128, 388), np.float32); rm[:, 0:384:32] = 0.0; rm[:, 384:] = 0.0
    c["c_rmask"] = rm
    j = np.arange(384)
    inwin = (j >= 127) & (j <= 255)
    bk = _t5_bucket(255 - j)
    oh = np.zeros((32, 384), np.float32)
    oh[bk[inwin], j[inwin]] = 1.0
    c["c_onehot"] = oh
    c["c_win"] = np.broadcast_to(inwin.astype(np.float32)[None, :], (64, 384)).copy()
    fm = np.ones((128, 256), np.float32)
    if first_half:
        fm[:, 0:112] = 0.0
    c["c_fmask"] = fm
    c["c_hsel"] = (np.arange(64)[:, None] // 8 == np.arange(8)[None, :]).astype(np.float32)
    return c


def _fm(vec):
    return np.ascontiguousarray(np.asarray(vec, np.float32).reshape(32, 128).T)


_CACHE = {}


def _prep(x_prompt, x_sample, state_hgrn, cache_k_win, cache_v_win, meta_tokens, rel_bias, hg_lower_bounds, hg_norm,
           hg_w_in, hg_onorm, hg_w_out, sw_norm, sw_w_in, sw_sinks, sw_w_out, final_norm):
    f32 = lambda a: np.asarray(a, dtype=np.float32)
    x_prompt = f32(x_prompt); x_sample = f32(x_sample); state_hgrn = f32(state_hgrn)
    cache_k_win = f32(cache_k_win); cache_v_win = f32(cache_v_win)
    w0 = f32(hg_w_in)[0]
    w0in = np.ascontiguousarray(w0.reshape(32, 128, 128, 128).transpose(2, 1, 0, 3)).reshape(128, 128, D)
    w0out = np.ascontiguousarray(f32(hg_w_out)[0].reshape(32, 128, 8, 512).transpose(2, 1, 0, 3)).reshape(8, 128, 16384)
    cols = []
    for n in range(8):
        for gi in range(4):
            cols.append(np.arange(128) + (4 * n + gi) * 128)
        kc_ = 4096 + n * 64 + np.arange(64)
        cols.append(np.concatenate([kc_, kc_]))
        if n % 2 == 0:
            cols.append(4096 + 512 + (n // 2) * 128 + np.arange(128))
        for gi in range(4):
            cols.append(4096 + 1024 + (4 * n + gi) * 128 + np.arange(128))
    cols = np.concatenate(cols)
    w1 = f32(sw_w_in)[0][:, cols]
    w1in = np.ascontiguousarray(w1.reshape(32, 128, 76, 128).transpose(2, 1, 0, 3)).reshape(76, 128, D)
    w1out = np.ascontiguousarray(f32(sw_w_out)[0].reshape(32, 128, 8, 512).transpose(2, 1, 0, 3)).reshape(8, 128, 16384)
    lbr = f32(hg_lower_bounds)
    vecs = np.ascontiguousarray(np.concatenate([_fm(lbr[0]), _fm(lbr[1]), _fm(lbr[2]), _fm(f32(hg_norm)[0]), _fm(f32(hg_onorm)[0]),
                                                _fm(f32(sw_norm)[0])], axis=1))
    shared = dict(w0in=w0in, w0out=w0out, w1in=w1in, w1out=w1out, vecs=vecs, fnorm=f32(final_norm).reshape(1, D),
                  relb=f32(rel_bias), sinkc=f32(sw_sinks)[0].reshape(64, 1), sinkr=f32(sw_sinks)[0].reshape(1, 64))
    cA = _consts(True); cB = _consts(False)
    meta = f32(meta_tokens)
    in_maps = []
    for c in range(8):
        s, half = c // 2, c % 2
        xp = np.concatenate([np.zeros((112, D), np.float32), meta, x_prompt[s]], axis=0)
        mp = dict(shared)
        mp.update(cA if half == 0 else cB)
        if half == 0:
            mp["xm"] = np.ascontiguousarray(xp[0:NT]); mp["xpre"] = np.zeros((NPRE, D), np.float32)
        else:
            mp["xm"] = np.ascontiguousarray(xp[1024:1024 + NT]); mp["xpre"] = np.ascontiguousarray(xp[0:NPRE])
        mp["xs"] = np.ascontiguousarray(x_sample[4 * c:4 * c + 4, 0, :])
        mp["st_in"] = np.ascontiguousarray(state_hgrn[0, 4 * c:4 * c + 4])
        mp["ck"] = np.ascontiguousarray(cache_k_win[0, 4 * c:4 * c + 4]).reshape(NS, 128, 512)
        mp["cv"] = np.ascontiguousarray(cache_v_win[0, 4 * c:4 * c + 4]).reshape(NS, 128, 512)
        in_maps.append(mp)
    return in_maps


def kernel(**inputs):
    in_maps = _prep(**inputs)
    if "nc" not in _CACHE:
        _CACHE["nc"] = build_program()[0]
    nc = _CACHE["nc"]
    res = run_bass_kernel_spmd(nc, in_maps, core_ids=list(range(8))).results
    y_prompt = np.zeros((4, 2048, D), np.float32); y_sample = np.zeros((32, 1, D), np.float32)
    nsp = np.zeros((1, 4, 32, 128, 128), np.float32); nkp = np.zeros((1, 4, 128, 8, 64), np.float32)
    nvp = np.zeros((1, 4, 128, 8, 64), np.float32); nss = np.zeros((1, 32, 32, 128, 128), np.float32)
    nks = np.zeros((1, 32, 128, 8, 64), np.float32); nvs = np.zeros((1, 32, 128, 8, 64), np.float32)
    for c in range(8):
        s, half = c // 2, c % 2
        r = res[c]
        y_prompt[s, half * 1024:(half + 1) * 1024] = r["y"]
        y_sample[4 * c:4 * c + 4, 0] = r["ys"]
        if half == 1:
            nsp[0, s] = r["stp"]; nkp[0, s] = r["kp"].reshape(128, 8, 64); nvp[0, s] = r["vp"].reshape(128, 8, 64)
        nss[0, 4 * c:4 * c + 4] = r["sts"]
        nks[0, 4 * c:4 * c + 4] = r["ks"].reshape(NS, 128, 8, 64); nvs[0, 4 * c:4 * c + 4] = r["vs"].reshape(NS, 128, 8, 64)
    return (y_prompt, y_sample, nsp, nkp, nvp, nss, nks, nvs)
```
